# Optimizing a Trainium2 kernel written in Bass

```python
import jax, jax.numpy as jnp
from jax import lax
import numpy as np

D_MODEL = 1024
BATCH = 2
SEQ = 16384
DEPTH = 2

HEAD_DIM = 64
MIX_WIDTH = D_MODEL
MOBA_HEADS = 6
MLSTM_HEADS = 4
SWA_Q_HEADS = 6
SWA_KV_HEADS = 2
MOBA_WIDTH = MOBA_HEADS * HEAD_DIM
MLSTM_WIDTH = MLSTM_HEADS * HEAD_DIM
SWA_Q_WIDTH = SWA_Q_HEADS * HEAD_DIM
SWA_KV_WIDTH = SWA_KV_HEADS * HEAD_DIM
MOBA_BLOCK = 256
MOBA_TOPK = 3
MOBA_Q_CHUNK = 64
MLSTM_CHUNK = 64
CONV_WIDTH = 4
SWA_WINDOW = 128
ROPE_THETA = 10000.0
D_FF = 4 * D_MODEL
NORM_EPS = 1e-6
IN_SPLITS = (MOBA_WIDTH, MOBA_WIDTH, MOBA_WIDTH, SWA_Q_WIDTH, SWA_KV_WIDTH, SWA_KV_WIDTH, 2 * MLSTM_WIDTH, MLSTM_WIDTH, MLSTM_WIDTH, MLSTM_HEADS, MLSTM_HEADS)
IN_WIDTH = sum(IN_SPLITS)

kernel_name = 'hybrid_moba_mlstm_swa_block'


def rms_norm(x, g):
    xf = x.astype(jnp.float32)
    y = xf * lax.rsqrt(jnp.mean(xf * xf, axis=-1, keepdims=True) + NORM_EPS)
    return (y * g.astype(jnp.float32)).astype(x.dtype)


def rope_tables(seq):
    inv = ROPE_THETA ** (-jnp.arange(0, HEAD_DIM, 2, dtype=jnp.float32) / HEAD_DIM)
    ang = jnp.arange(seq, dtype=jnp.float32)[:, None] * inv[None, :]
    return jnp.cos(ang), jnp.sin(ang)


def apply_rope(x, cos, sin):
    c = cos[:, None, :].astype(x.dtype)
    s = sin[:, None, :].astype(x.dtype)
    x1, x2 = jnp.split(x, 2, axis=-1)
    return jnp.concatenate([x1 * c - x2 * s, x2 * c + x1 * s], axis=-1)


def causal_depthwise_conv(x, w, b):
    C = x.shape[-1]
    y = lax.conv_general_dilated(x, w[:, None, :].astype(x.dtype), (1,), [(w.shape[0] - 1, 0)],
                                 dimension_numbers=('NWC', 'WIO', 'NWC'), feature_group_count=C)
    return y + b.astype(x.dtype)


def moba_attention(q, k, v):
    B, H, S, D = q.shape
    L = MOBA_BLOCK
    n_blocks = -(-S // L)
    s_pad = n_blocks * L
    if s_pad != S:
        padw = ((0, 0), (0, 0), (0, s_pad - S), (0, 0))
        q, k, v = jnp.pad(q, padw), jnp.pad(k, padw), jnp.pad(v, padw)
    k_blk = k.reshape(B, H, n_blocks, L, D)
    v_blk = v.reshape(B, H, n_blocks, L, D)
    k_mean = jnp.mean(k_blk.astype(jnp.float32), axis=3)
    top = min(MOBA_TOPK, n_blocks)
    scale = D ** -0.5
    Qc = MOBA_Q_CHUNK
    b_idx = jnp.arange(B)[:, None, None, None]
    h_idx = jnp.arange(H)[None, :, None, None]

    def one_chunk(c):
        start = c * Qc
        blk = start // L
        qc = lax.dynamic_slice_in_dim(q, start, Qc, axis=2)
        gate = jnp.einsum('bhqd,bhnd->bhqn', qc.astype(jnp.float32), k_mean)
        gate = jnp.where(jnp.arange(n_blocks) < blk, gate, -jnp.inf)
        _, sel = lax.top_k(gate, top)
        sel_ok = jnp.arange(top) < blk
        k_sel = k_blk[b_idx, h_idx, sel]
        v_sel = v_blk[b_idx, h_idx, sel]
        s_sel = jnp.einsum('bhqd,bhqrld->bhqrl', qc, k_sel).astype(jnp.float32) * scale
        s_sel = jnp.where(sel_ok[:, None], s_sel, -jnp.inf).reshape(B, H, Qc, top * L)
        k_own = lax.dynamic_slice_in_dim(k, blk * L, L, axis=2)
        v_own = lax.dynamic_slice_in_dim(v, blk * L, L, axis=2)
        s_own = jnp.einsum('bhqd,bhld->bhql', qc, k_own).astype(jnp.float32) * scale
        q_pos = start + jnp.arange(Qc)
        k_pos = blk * L + jnp.arange(L)
        s_own = jnp.where(k_pos[None, :] <= q_pos[:, None], s_own, -jnp.inf)
        p = jax.nn.softmax(jnp.concatenate([s_sel, s_own], axis=-1), axis=-1).astype(v.dtype)
        p_sel = p[..., :top * L].reshape(B, H, Qc, top, L)
        p_own = p[..., top * L:]
        return (jnp.einsum('bhqrl,bhqrld->bhqd', p_sel, v_sel)
                + jnp.einsum('bhql,bhld->bhqd', p_own, v_own))

    outs = lax.map(one_chunk, jnp.arange(s_pad // Qc))
    out = jnp.moveaxis(outs, 0, 2).reshape(B, H, s_pad, D)
    return out[:, :, :S]


def mlstm_chunkwise(q, k, v, i_pre, f_pre):
    B, H, S, D = q.shape
    L = MLSTM_CHUNK
    nc = S // L
    k = k * (D ** -0.5)
    log_f = jax.nn.log_sigmoid(f_pre)

    def chunks(a):
        return jnp.moveaxis(a.reshape(B, H, nc, L, *a.shape[3:]), 2, 0)

    causal = jnp.tril(jnp.ones((L, L), dtype=bool))

    def step(carry, inp):
        C, n, m = carry
        qc, kc, vc, ic, lfc = inp
        b = jnp.cumsum(lfc, axis=-1)
        a = b[..., -1]
        d = b[..., :, None] - b[..., None, :] + ic[..., None, :]
        d = jnp.where(causal, d, -jnp.inf)
        inter = b + m[..., None]
        m_t = jnp.maximum(inter, jnp.max(d, axis=-1))
        w_intra = jnp.exp(d - m_t[..., None])
        w_inter = jnp.exp(inter - m_t)
        qk = jnp.einsum('bhtd,bhsd->bhts', qc, kc) * w_intra
        num = (w_inter[..., None] * jnp.einsum('bhtd,bhde->bhte', qc, C)
               + jnp.einsum('bhts,bhse->bhte', qk, vc))
        den = w_inter * jnp.einsum('bhtd,bhd->bht', qc, n) + jnp.sum(qk, axis=-1)
        h = num / jnp.maximum(jnp.abs(den), jnp.exp(-m_t))[..., None]
        g = a[..., None] - b + ic
        m_new = jnp.maximum(a + m, jnp.max(g, axis=-1))
        w_s = jnp.exp(g - m_new[..., None])
        decay = jnp.exp(a + m - m_new)
        C_new = decay[..., None, None] * C + jnp.einsum('bhsd,bhse->bhde', kc * w_s[..., None], vc)
        n_new = decay[..., None] * n + jnp.einsum('bhs,bhsd->bhd', w_s, kc)
        return (C_new, n_new, m_new), h

    init = (jnp.zeros((B, H, D, D), jnp.float32), jnp.zeros((B, H, D), jnp.float32),
            jnp.zeros((B, H), jnp.float32))
    _, h = lax.scan(step, init, (chunks(q), chunks(k), chunks(v), chunks(i_pre), chunks(log_f)))
    return jnp.moveaxis(h, 0, 2).reshape(B, H, S, D)


def swa_attention(q, k, v, sinks):
    B, S, Hq, D = q.shape
    Hkv = k.shape[2]
    G = Hq // Hkv
    W = SWA_WINDOW
    nb = S // W
    qb = q.reshape(B, nb, W, Hkv, G, D)

    def band(a):
        ab = a.reshape(B, nb, W, Hkv, D)
        prev = jnp.pad(ab, ((0, 0), (1, 0), (0, 0), (0, 0), (0, 0)))[:, :-1]
        return jnp.concatenate([prev, ab], axis=2)

    kb, vb = band(k), band(v)
    s = jnp.einsum('bnqkgd,bnjkd->bnkgqj', qb, kb).astype(jnp.float32) * (D ** -0.5)
    qi = jnp.arange(W)[:, None] + W
    kj = jnp.arange(2 * W)[None, :]
    in_window = (qi - kj >= 0) & (qi - kj < W)
    exists = (jnp.arange(nb)[:, None, None] > 0) | (kj[None] >= W)
    mask = in_window[None] & exists
    s = jnp.where(mask[None, :, None, None], s, -jnp.inf)
    sink = jnp.broadcast_to(sinks.astype(jnp.float32).reshape(1, 1, Hkv, G, 1, 1), s.shape[:-1] + (1,))
    p = jax.nn.softmax(jnp.concatenate([s, sink], axis=-1), axis=-1)[..., :-1].astype(v.dtype)
    o = jnp.einsum('bnkgqj,bnjkd->bnqkgd', p, vb)
    return o.reshape(B, S, Hq, D)


def hybrid_layer(x, cos, sin, ln1, w_in, conv_w, conv_b, igate_b, fgate_b, mlstm_norm,
                 moba_q_norm, moba_k_norm, swa_q_norm, swa_k_norm, swa_sinks, w_out,
                 ln2, w_up, w_down):
    B, S, _ = x.shape
    hn = rms_norm(x, ln1)
    proj = hn @ w_in
    idx = np.cumsum(IN_SPLITS)[:-1].tolist()
    mq, mk, mv, sq, sk, sv, xqk, xv, xo, xi, xf = jnp.split(proj, idx, axis=-1)

    def heads(a):
        return a.reshape(B, S, -1, HEAD_DIM)

    def bhsd(a):
        return a.transpose(0, 2, 1, 3)

    mq = apply_rope(rms_norm(heads(mq), moba_q_norm), cos, sin)
    mk = apply_rope(rms_norm(heads(mk), moba_k_norm), cos, sin)
    y_moba = bhsd(moba_attention(bhsd(mq), bhsd(mk), bhsd(heads(mv)))).reshape(B, S, MOBA_WIDTH)

    qk = jax.nn.silu(causal_depthwise_conv(xqk, conv_w, conv_b))
    lq, lk = jnp.split(qk, 2, axis=-1)
    i_pre = (xi + igate_b).astype(jnp.float32).transpose(0, 2, 1)
    f_pre = (xf + fgate_b).astype(jnp.float32).transpose(0, 2, 1)
    h = mlstm_chunkwise(bhsd(heads(lq)).astype(jnp.float32), bhsd(heads(lk)).astype(jnp.float32),
                        bhsd(heads(xv)).astype(jnp.float32), i_pre, f_pre)
    h = bhsd(h).astype(x.dtype) * heads(jax.nn.sigmoid(xo))
    y_mlstm = rms_norm(h, mlstm_norm).reshape(B, S, MLSTM_WIDTH)

    sq = apply_rope(rms_norm(heads(sq), swa_q_norm), cos, sin)
    sk = apply_rope(rms_norm(heads(sk), swa_k_norm), cos, sin)
    y_swa = swa_attention(sq, sk, heads(sv), swa_sinks).reshape(B, S, SWA_Q_WIDTH)

    x = x + jnp.concatenate([y_moba, y_mlstm, y_swa], axis=-1) @ w_out
    u = rms_norm(x, ln2) @ w_up
    return x + jnp.square(jax.nn.relu(u)) @ w_down


def setup_inputs(seed: int = 0) -> dict:
    key = jax.random.key(seed)
    ks = jax.random.split(key, 17)
    f32 = jnp.float32

    def nrm(k, shape, scale):
        return jax.random.normal(k, shape, f32) * scale

    return {
        'x': nrm(ks[0], (BATCH, SEQ, D_MODEL), 1.0),
        'ln1': 1.0 + nrm(ks[1], (DEPTH, D_MODEL), 0.02),
        'w_in': nrm(ks[2], (DEPTH, D_MODEL, IN_WIDTH), D_MODEL ** -0.5),
        'conv_w': nrm(ks[3], (DEPTH, CONV_WIDTH, 2 * MLSTM_WIDTH), CONV_WIDTH ** -0.5),
        'conv_b': nrm(ks[4], (DEPTH, 2 * MLSTM_WIDTH), 0.02),
        'igate_b': nrm(ks[5], (DEPTH, MLSTM_HEADS), 0.1),
        'fgate_b': jnp.linspace(3.0, 6.0, MLSTM_HEADS, dtype=f32)[None] + nrm(ks[6], (DEPTH, MLSTM_HEADS), 0.1),
        'mlstm_norm': 1.0 + nrm(ks[7], (DEPTH, MLSTM_HEADS, HEAD_DIM), 0.02),
        'moba_q_norm': 1.0 + nrm(ks[8], (DEPTH, HEAD_DIM), 0.02),
        'moba_k_norm': 1.0 + nrm(ks[9], (DEPTH, HEAD_DIM), 0.02),
        'swa_q_norm': 1.0 + nrm(ks[10], (DEPTH, HEAD_DIM), 0.02),
        'swa_k_norm': 1.0 + nrm(ks[11], (DEPTH, HEAD_DIM), 0.02),
        'swa_sinks': nrm(ks[12], (DEPTH, SWA_Q_HEADS), 1.0),
        'w_out': nrm(ks[13], (DEPTH, MIX_WIDTH, D_MODEL), MIX_WIDTH ** -0.5),
        'ln2': 1.0 + nrm(ks[14], (DEPTH, D_MODEL), 0.02),
        'w_up': nrm(ks[15], (DEPTH, D_MODEL, D_FF), D_MODEL ** -0.5),
        'w_down': nrm(ks[16], (DEPTH, D_FF, D_MODEL), D_FF ** -0.5),
    }


def reference(x, ln1, w_in, conv_w, conv_b, igate_b, fgate_b, mlstm_norm, moba_q_norm,
              moba_k_norm, swa_q_norm, swa_k_norm, swa_sinks, w_out, ln2, w_up, w_down):
    cos, sin = rope_tables(x.shape[1])
    for l in range(DEPTH):
        x = hybrid_layer(x, cos, sin, ln1[l], w_in[l], conv_w[l], conv_b[l], igate_b[l], fgate_b[l],
                         mlstm_norm[l], moba_q_norm[l], moba_k_norm[l], swa_q_norm[l], swa_k_norm[l],
                         swa_sinks[l], w_out[l], ln2[l], w_up[l], w_down[l])
    return x
```

```python
import contextlib
import os
import numpy as np
import ml_dtypes
import concourse.bass as bass
import concourse.mybir as mybir
from concourse.bass_utils import run_bass_kernel_spmd
from concourse.alu_op_type import AluOpType as ALU

AF = mybir.ActivationFunctionType
AX = mybir.AxisListType
F32 = mybir.dt.float32
BF16 = mybir.dt.bfloat16
NPBF = ml_dtypes.bfloat16

NCORE = 8
S = 16384
NT = 4096
DM = 1024
INW = 2824
EPS = 1e-6
BIG = 30000.0


class Buf:
    __slots__ = ("name", "w", "r", "sem", "cnt", "excl")

    def __init__(self, name, excl=False):
        self.name = name
        self.excl = excl
        self.w = None
        self.r = {}
        self.sem = None
        self.cnt = 0


class Prog:
    ENG = ("pe", "dve", "act", "pool", "sp")
    BLK = dict(pe="tensor", dve="vector", act="scalar", pool="gpsimd", sp="sync")

    def __init__(self, nc):
        self.nc = nc
        self.stack = contextlib.ExitStack()
        self.stream = {e: [] for e in self.ENG}
        self.esem = {e: nc.alloc_semaphore(name="es_" + e) for e in self.ENG}
        self.ecnt = {e: 0 for e in self.ENG}
        self.seen = {e: {} for e in self.ENG}
        self.nbuf = 0
        self.dmabufs = []
        self.engo = dict(pe=nc.tensor, dve=nc.vector, act=nc.scalar, pool=nc.gpsimd, sp=nc.sync)

    def sbuf(self, name, shape, dtype):
        return self.stack.enter_context(self.nc.sbuf_tensor("sb_" + name, list(shape), dtype))

    def psum(self, name, shape, dtype):
        return self.stack.enter_context(self.nc.psum_tensor("pp_" + name, list(shape), dtype))

    def buf(self, name=None, excl=False):
        self.nbuf += 1
        return Buf(name or f"b{self.nbuf}", excl)

    def pbuf(self, name=None):
        return self.buf(name, True)

    def tile(self, name, shape, dtype):
        return self.sbuf(name, shape, dtype), self.buf(name)

    def _waits(self, e, reads, writes):
        need = {}

        def add(ev, raw):
            if ev is None:
                return
            sem, val, eng = ev
            if eng == e and not raw:
                return
            k = id(sem)
            if self.seen[e].get(k, 0) >= val:
                return
            if k not in need or need[k][1] < val:
                need[k] = (sem, val)

        for b in reads:
            add(b.w, True)
            if b.excl:
                for ev in b.r.values():
                    add(ev, False)
        for b in writes:
            add(b.w, False)
            for ev in b.r.values():
                add(ev, False)
        for k, (sem, val) in need.items():
            self.seen[e][k] = val
        return list(need.values())

    def _record(self, ev, reads, writes):
        k = id(ev[0])
        for b in reads:
            b.r[k] = ev
        for b in writes:
            b.w = ev
            b.r = {}

    def _emit(self, e, waits, fn, inc):
        eng = self.engo[e]
        for sem, val in waits:
            eng.wait_ge(sem, val)
        if fn is not None:
            fn(eng).then_inc(inc[0], inc[1])

    def op(self, e, fn, reads=(), writes=()):
        waits = self._waits(e, reads, writes)
        self.ecnt[e] += 1
        ev = (self.esem[e], self.ecnt[e], e)
        self._emit(e, waits, fn, (self.esem[e], 1))
        self._record(ev, reads, writes)

    def dma(self, e, out, in_, reads=(), writes=(), sembuf=None, **kw):
        waits = self._waits(e, reads, writes)
        sb = sembuf or writes[0]
        if sb.sem is None:
            sb.sem = self.nc.alloc_semaphore(name="ds_" + sb.name)
            self.dmabufs.append(sb)
        sb.cnt += 16
        ev = (sb.sem, sb.cnt, "dma")
        self._emit(e, waits, lambda eng: eng.dma_start(out=out, in_=in_, **kw), (sb.sem, 16))
        self._record(ev, reads, writes)

    def wait_all(self, e, bufs):
        waits = self._waits(e, bufs, ())
        self._emit(e, waits, None, None)

    def emit(self):
        eng = self.engo["sp"]
        for e in self.ENG:
            if self.ecnt[e] > 0:
                eng.wait_ge(self.esem[e], self.ecnt[e])
        for b in self.dmabufs:
            eng.wait_ge(b.sem, b.cnt)
        for e in self.ENG:
            eng.sem_clear(self.esem[e])
        for b in self.dmabufs:
            eng.sem_clear(b.sem)
        self.stack.close()


def new_nc():
    return bass.Bass("TRN2", target_bir_lowering=False)


def dram(nc, name, shape, dt, kind):
    return nc.dram_tensor(name, list(shape), dt, kind=kind).ap()


def make_ident(p, name="ident"):
    ident, b_id = p.tile(name, [128, 128], BF16)
    p.op("pool", lambda e: e.memset(ident[:], 0.0), writes=[b_id])
    p.op("pool", lambda e: e.affine_select(out=ident[:], in_=ident[:], pattern=[[-1, 128]], compare_op=ALU.not_equal,
                                           fill=1.0, base=0, channel_multiplier=1), reads=[b_id], writes=[b_id])
    return ident, b_id


def load_w_bf16(p, dst, b_dst, src, nk):
    n = src.shape[1]
    for k in range(nk):
        for c0 in range(0, n, 2048):
            cw = min(2048, n - c0)
            p.dma("pool", dst[:, k, c0:c0 + cw], src[k * 128:(k + 1) * 128, c0:c0 + cw], writes=[b_dst])


def rms_rstd(p, ss, b_ss, rs, b_rs, n):
    p.op("dve", lambda e: e.tensor_scalar(out=rs, in0=ss, scalar1=1.0 / n, scalar2=EPS, op0=ALU.mult, op1=ALU.add),
         reads=[b_ss], writes=[b_rs])
    p.op("act", lambda e: e.activation(out=rs, in_=rs, func=AF.Sqrt), reads=[b_rs], writes=[b_rs])
    p.op("dve", lambda e: e.reciprocal(out=rs, in_=rs), reads=[b_rs], writes=[b_rs])


NORMED = [
    ("mq", 0, 6, 0, 0), ("mk", 384, 6, 1, 384), ("sq", 1152, 6, 2, 1152), ("sk", 1536, 2, 3, 1536)]
PLAIN = [(768, 384, 768), (1664, 128, 1664), (2304, 256, 1792)]


def build_A():
    nc = new_nc()
    x = dram(nc, "x", [NT, DM], F32, "ExternalInput")
    w_in = dram(nc, "w_in", [DM, INW], F32, "ExternalInput")
    ln = dram(nc, "ln", [DM], F32, "ExternalInput")
    gains = dram(nc, "gains", [4, 384], F32, "ExternalInput")
    cos6 = dram(nc, "cos6", [NT, 192], F32, "ExternalInput")
    sin6 = dram(nc, "sin6", [NT, 192], F32, "ExternalInput")
    obf = dram(nc, "obf", [NT, 2048], BF16, "ExternalOutput")
    of32 = dram(nc, "of32", [NT, 776], F32, "ExternalOutput")
    p = Prog(nc)
    ws, b_ws = p.tile("ws", [128, 8, INW], BF16)
    load_w_bf16(p, ws, b_ws, w_in, 8)
    ident, b_id = make_ident(p)
    lnb, b_lnb = p.tile("lnb", [128, DM], F32)
    p.dma("sp", lnb[:], ln.partition_broadcast(128), writes=[b_lnb])
    G, b_G = p.tile("G", [128, 4, 384], F32)
    for j in range(4):
        p.dma("sp", G[:, j, :], gains[j].partition_broadcast(128), writes=[b_G])
    xs = [p.tile(f"xs{j}", [128, DM], F32) for j in range(2)]
    cs = [p.tile(f"cs{j}", [128, 192], F32) for j in range(2)]
    sn = [p.tile(f"sn{j}", [128, 192], F32) for j in range(2)]
    junk, b_junk = p.tile("junk", [128, DM], BF16)
    ss, b_ss = p.tile("ss", [128, 1], F32)
    rs, b_rs = p.tile("rs", [128, 1], F32)
    hn, b_hn = p.tile("hn", [128, DM], BF16)
    xT = [p.tile(f"xT{j}", [128, 8, 128], BF16) for j in range(2)]
    proj = [p.tile(f"proj{j}", [128, INW], F32) for j in range(2)]
    outb = [p.tile(f"outb{j}", [128, 2048], BF16) for j in range(2)]
    ps_t = p.psum("ps_t", [128, 1024], BF16); b_pst = p.pbuf("pst")
    ps_o = [(p.psum(f"ps_o{j}", [128, 512], F32), p.pbuf(f"pso{j}")) for j in range(3)]
    tmp = {}
    for nm, _, nh, _, _ in NORMED:
        d = {}
        for t, w in (("ssq", nh), ("rst", nh)):
            d[t] = p.tile(f"{nm}_{t}", [128, w], F32)
        d["jk"] = p.tile(f"{nm}_jk", [128, 64], F32)
        d["tg"] = p.tile(f"{nm}_tg", [128, nh * 64], F32)
        d["o"] = p.tile(f"{nm}_o", [128, nh * 64], F32)
        for t in ("a", "b", "d", "e"):
            d[t] = p.tile(f"{nm}_{t}", [128, nh * 32], F32)
        tmp[nm] = d
    b_obf = p.buf("obf"); b_of32 = p.buf("of32")
    nt = NT // 128

    def front(i):
        xsi, b_xs = xs[i % 2]
        p.dma("sp", xsi[:], x[i * 128:(i + 1) * 128, :], writes=[b_xs])
        p.dma("sp", cs[i % 2][0][:], cos6[i * 128:(i + 1) * 128, :], writes=[cs[i % 2][1]])
        p.dma("sp", sn[i % 2][0][:], sin6[i * 128:(i + 1) * 128, :], writes=[sn[i % 2][1]])
        p.op("act", lambda e: e.activation(out=junk[:], in_=xsi[:], func=AF.Square, accum_out=ss[:]),
             reads=[b_xs], writes=[b_junk, b_ss])
        rms_rstd(p, ss[:], b_ss, rs[:], b_rs, DM)
        p.op("dve", lambda e: e.scalar_tensor_tensor(out=hn[:], in0=xsi[:], scalar=rs[:, 0:1], in1=lnb[:],
                                                     op0=ALU.mult, op1=ALU.mult),
             reads=[b_xs, b_rs, b_lnb], writes=[b_hn])
        for k in range(8):
            p.op("pe", lambda e, k=k: e.transpose(out=ps_t[:, k * 128:(k + 1) * 128], in_=hn[:, k * 128:(k + 1) * 128],
                                                  identity=ident[:]), reads=[b_hn, b_id], writes=[b_pst])
        xTi, b_xT = xT[i % 2]
        p.op("act", lambda e: e.copy(out=xTi[:].rearrange("p k t -> p (k t)"), in_=ps_t[:]), reads=[b_pst], writes=[b_xT])

    def back(i):
        xTi, b_xT = xT[i % 2]
        pj, b_pj = proj[i % 2]
        ob, b_ob = outb[i % 2]
        for n in range(6):
            c0 = n * 512
            cw = min(512, INW - c0)
            ps, b_ps = ps_o[n % 3]
            for k in range(8):
                p.op("pe", lambda e, k=k, ps=ps, c0=c0, cw=cw: e.matmul(ps[:, :cw], lhsT=xTi[:, k, :], rhs=ws[:, k, c0:c0 + cw],
                                                                          start=(k == 0), stop=(k == 7)),
                     reads=[b_xT, b_ws], writes=[b_ps])
            if n % 2 == 0:
                p.op("dve", lambda e, ps=ps, c0=c0, cw=cw: e.tensor_copy(out=pj[:, c0:c0 + cw], in_=ps[:, :cw]),
                     reads=[b_ps], writes=[b_pj])
            else:
                p.op("act", lambda e, ps=ps, c0=c0, cw=cw: e.copy(out=pj[:, c0:c0 + cw], in_=ps[:, :cw]),
                     reads=[b_ps], writes=[b_pj])
        csi, b_cs = cs[i % 2]
        sni, b_sn = sn[i % 2]
        for nm, off, nh, gi, oo in NORMED:
            d = tmp[nm]
            (ssq, b_ssq), (rst, b_rst), (jk, b_jk) = d["ssq"], d["rst"], d["jk"]
            (tg, b_tg), (o, b_o) = d["tg"], d["o"]
            (a, b_a), (bb, b_b), (dd, b_d), (ee, b_e) = d["a"], d["b"], d["d"], d["e"]
            for h in range(nh):
                p.op("act", lambda e, h=h: e.activation(out=jk[:], in_=pj[:, off + h * 64: off + (h + 1) * 64], func=AF.Square,
                                                        accum_out=ssq[:, h:h + 1]), reads=[b_pj], writes=[b_jk, b_ssq])
            rms_rstd(p, ssq[:], b_ssq, rst[:], b_rst, 64)
            p.op("pool", lambda e: e.tensor_tensor(out=tg[:], in0=pj[:, off:off + nh * 64], in1=G[:, gi, 0:nh * 64], op=ALU.mult),
                 reads=[b_pj, b_G], writes=[b_tg])
            tgv = tg[:].rearrange("p (h t d) -> p h t d", h=nh, t=2)
            ov = o[:].rearrange("p (h t d) -> p h t d", h=nh, t=2)
            x1, x2 = tgv[:, :, 0, :], tgv[:, :, 1, :]
            c = csi[:, 0:nh * 32].rearrange("p (h d) -> p h d", h=nh)
            s_ = sni[:, 0:nh * 32].rearrange("p (h d) -> p h d", h=nh)
            v3 = lambda t: t[:].rearrange("p (h d) -> p h d", h=nh)
            p.op("dve", lambda e: e.tensor_tensor(out=v3(a), in0=x1, in1=c, op=ALU.mult), reads=[b_tg, b_cs], writes=[b_a])
            p.op("pool", lambda e: e.tensor_tensor(out=v3(bb), in0=x2, in1=s_, op=ALU.mult), reads=[b_tg, b_sn], writes=[b_b])
            p.op("dve", lambda e: e.tensor_tensor(out=ov[:, :, 0, :], in0=v3(a), in1=v3(bb), op=ALU.subtract),
                 reads=[b_a, b_b], writes=[b_o])
            p.op("pool", lambda e: e.tensor_tensor(out=v3(dd), in0=x2, in1=c, op=ALU.mult), reads=[b_tg, b_cs], writes=[b_d])
            p.op("dve", lambda e: e.tensor_tensor(out=v3(ee), in0=x1, in1=s_, op=ALU.mult), reads=[b_tg, b_sn], writes=[b_e])
            p.op("pool", lambda e: e.tensor_tensor(out=ov[:, :, 1, :], in0=v3(dd), in1=v3(ee), op=ALU.add),
                 reads=[b_d, b_e, b_o], writes=[b_o])
            p.op("dve", lambda e: e.tensor_tensor(out=ob[:, oo:oo + nh * 64].rearrange("p (h d) -> p h d", h=nh),
                                                  in0=o[:].rearrange("p (h d) -> p h d", h=nh),
                                                  in1=rst[:].unsqueeze(2).to_broadcast([128, nh, 64]), op=ALU.mult),
                 reads=[b_o, b_rst, b_ob], writes=[b_ob])
        for j, (po, w, oo) in enumerate(PLAIN):
            eng = "act" if j % 2 == 0 else "pool"
            if eng == "act":
                p.op("act", lambda e, po=po, w=w, oo=oo: e.copy(out=ob[:, oo:oo + w], in_=pj[:, po:po + w]),
                     reads=[b_pj, b_ob], writes=[b_ob])
            else:
                p.op("pool", lambda e, po=po, w=w, oo=oo: e.tensor_copy(out=ob[:, oo:oo + w], in_=pj[:, po:po + w]),
                     reads=[b_pj, b_ob], writes=[b_ob])
        r0 = i * 128
        p.dma("sp", obf[r0:r0 + 128, :], ob[:], reads=[b_ob], writes=[b_obf])
        p.dma("sp", of32[r0:r0 + 128, 0:512], pj[:, 1792:2304], reads=[b_pj], writes=[b_of32])
        p.dma("sp", of32[r0:r0 + 128, 512:776], pj[:, 2560:2824], reads=[b_pj], writes=[b_of32])

    front(0)
    for i in range(nt):
        if i + 1 < nt:
            front(i + 1)
        back(i)
    p.wait_all("sp", [b_obf, b_of32])
    p.emit()
    return nc


def build_C1():
    nc = new_nc()
    x = dram(nc, "x", [NT, DM], F32, "ExternalInput")
    yT = dram(nc, "yT", [DM, NT], BF16, "ExternalInput")
    w_out = dram(nc, "w_out", [DM, DM], F32, "ExternalInput")
    ln = dram(nc, "ln", [DM], F32, "ExternalInput")
    x1o = dram(nc, "x1", [NT, DM], F32, "ExternalOutput")
    hT = dram(nc, "hT", [DM, NT], BF16, "ExternalOutput")
    p = Prog(nc)
    wo, b_wo = p.tile("wo", [128, 8, DM], BF16)
    load_w_bf16(p, wo, b_wo, w_out, 8)
    ident, b_id = make_ident(p)
    lnb, b_lnb = p.tile("lnb", [128, DM], F32)
    p.dma("sp", lnb[:], ln.partition_broadcast(128), writes=[b_lnb])
    xs = [p.tile(f"xs{j}", [128, DM], F32) for j in range(2)]
    yt = [p.tile(f"yt{j}", [128, 8, 128], BF16) for j in range(2)]
    x1 = [p.tile(f"x1{j}", [128, DM], F32) for j in range(2)]
    junk, b_junk = p.tile("junk", [128, DM], BF16)
    ss, b_ss = p.tile("ss", [128, 1], F32)
    rs, b_rs = p.tile("rs", [128, 1], F32)
    hn, b_hn = p.tile("hn", [128, DM], BF16)
    hTt = [p.tile(f"hTt{j}", [128, 8, 128], BF16) for j in range(2)]
    ps_t = p.psum("ps_t", [128, 1024], BF16); b_pst = p.pbuf("pst")
    ps_o = [(p.psum(f"ps_o{j}", [128, 512], F32), p.pbuf(f"pso{j}")) for j in range(4)]
    b_x1o = p.buf("x1o"); b_hT = p.buf("hTo")
    yTv = yT.rearrange("(k p) t -> p k t", p=128)
    hTv = hT.rearrange("(k p) t -> p k t", p=128)
    for i in range(NT // 128):
        xsi, b_xs = xs[i % 2]
        yti, b_yt = yt[i % 2]
        x1i, b_x1 = x1[i % 2]
        hti, b_ht = hTt[i % 2]
        p.dma("sp", xsi[:], x[i * 128:(i + 1) * 128, :], writes=[b_xs])
        p.dma("sp", yti[:], yTv[:, :, i * 128:(i + 1) * 128], writes=[b_yt])
        for n in range(2):
            ps, b_ps = ps_o[(2 * i + n) % 4]
            for k in range(8):
                p.op("pe", lambda e, k=k, ps=ps, n=n: e.matmul(ps[:], lhsT=yti[:, k, :], rhs=wo[:, k, n * 512:(n + 1) * 512],
                                                                 start=(k == 0), stop=(k == 7)), reads=[b_yt, b_wo], writes=[b_ps])
            p.op("dve", lambda e, ps=ps, n=n: e.tensor_tensor(out=x1i[:, n * 512:(n + 1) * 512], in0=ps[:],
                                                              in1=xsi[:, n * 512:(n + 1) * 512], op=ALU.add),
                 reads=[b_ps, b_xs, b_x1], writes=[b_x1])
        p.dma("sp", x1o[i * 128:(i + 1) * 128, :], x1i[:], reads=[b_x1], writes=[b_x1o])
        p.op("act", lambda e: e.activation(out=junk[:], in_=x1i[:], func=AF.Square, accum_out=ss[:]),
             reads=[b_x1], writes=[b_junk, b_ss])
        rms_rstd(p, ss[:], b_ss, rs[:], b_rs, DM)
        p.op("dve", lambda e: e.scalar_tensor_tensor(out=hn[:], in0=x1i[:], scalar=rs[:, 0:1], in1=lnb[:],
                                                     op0=ALU.mult, op1=ALU.mult), reads=[b_x1, b_rs, b_lnb], writes=[b_hn])
        for k in range(8):
            p.op("pe", lambda e, k=k: e.transpose(out=ps_t[:, k * 128:(k + 1) * 128], in_=hn[:, k * 128:(k + 1) * 128],
                                                  identity=ident[:]), reads=[b_hn, b_id], writes=[b_pst])
        p.op("act", lambda e: e.copy(out=hti[:].rearrange("p k t -> p (k t)"), in_=ps_t[:]), reads=[b_pst], writes=[b_ht])
        p.dma("sp", hTv[:, :, i * 128:(i + 1) * 128], hti[:], reads=[b_ht], writes=[b_hT])
    p.wait_all("sp", [b_x1o, b_hT])
    p.emit()
    return nc


def build_C2():
    nc = new_nc()
    x1 = dram(nc, "x1", [NT, DM], F32, "ExternalInput")
    hT = dram(nc, "hT", [DM, NT], BF16, "ExternalInput")
    w_up = dram(nc, "w_up", [DM, 4096], F32, "ExternalInput")
    w_dn = dram(nc, "w_dn", [4096, DM], F32, "ExternalInput")
    x2 = dram(nc, "x2", [NT, DM], F32, "ExternalOutput")
    xm = dram(nc, "xmid", [NT, DM], F32, "Internal")
    p = Prog(nc)
    FH = 2048
    wu, b_wu = p.tile("wu", [128, 8, FH], BF16)
    wd, b_wd = p.tile("wd", [128, 16, DM], BF16)
    hTv = hT.rearrange("(k p) t -> p k t", p=128)
    ht = [p.tile(f"ht{j}", [128, 8, 512], BF16) for j in range(2)]
    uT = [p.tile(f"uT{j}", [128, 16, 512], BF16) for j in range(2)]
    rl = [p.tile(f"rl{j}", [128, 512], F32) for j in range(2)]
    xs = [p.tile(f"xs{j}", [128, DM], F32) for j in range(2)]
    xo = [p.tile(f"xo{j}", [128, DM], F32) for j in range(2)]
    ps_u = [(p.psum(f"ps_u{j}", [128, 512], F32), p.pbuf(f"psu{j}")) for j in range(4)]
    ps_d = [(p.psum(f"ps_d{j}", [128, 512], F32), p.pbuf(f"psd{j}")) for j in range(4)]
    b_mid = [p.buf(f"mid{i}") for i in range(NT // 128)]
    b_x2 = p.buf("x2")
    cnt = 0
    for ps_ in range(2):
        load_w_bf16(p, wu, b_wu, w_up[:, ps_ * FH:(ps_ + 1) * FH], 8)
        load_w_bf16(p, wd, b_wd, w_dn[ps_ * FH:(ps_ + 1) * FH, :], 16)
        src = x1 if ps_ == 0 else xm
        dst = xm if ps_ == 0 else x2
        for g in range(NT // 512):
            hti, b_ht = ht[g % 2]
            uTi, b_uT = uT[g % 2]
            p.dma("sp", hti[:], hTv[:, :, g * 512:(g + 1) * 512], writes=[b_ht])
            for f in range(16):
                ps, b_ps = ps_u[f % 4]
                for k in range(8):
                    p.op("pe", lambda e, k=k, f=f, ps=ps: e.matmul(ps[:], lhsT=wu[:, k, f * 128:(f + 1) * 128], rhs=hti[:, k, :],
                                                                     start=(k == 0), stop=(k == 7)), reads=[b_wu, b_ht], writes=[b_ps])
                r, b_r = rl[f % 2]
                p.op("act", lambda e, ps=ps, r=r: e.activation(out=r[:], in_=ps[:], func=AF.Relu), reads=[b_ps], writes=[b_r])
                eng = "dve" if f % 2 == 0 else "pool"
                p.op(eng, lambda e, r=r, f=f: e.tensor_tensor(out=uTi[:, f, :], in0=r[:], in1=r[:], op=ALU.mult),
                     reads=[b_r, b_uT], writes=[b_uT])
            for tt in range(4):
                i = g * 4 + tt
                xsi, b_xs = xs[i % 2]
                xoi, b_xo = xo[i % 2]
                p.dma("sp", xsi[:], src[i * 128:(i + 1) * 128, :], reads=([b_mid[i]] if ps_ == 1 else []), writes=[b_xs])
                for n in range(2):
                    ps, b_ps = ps_d[cnt % 4]
                    cnt += 1
                    for f in range(16):
                        p.op("pe", lambda e, f=f, ps=ps, n=n, tt=tt: e.matmul(ps[:], lhsT=uTi[:, f, tt * 128:(tt + 1) * 128],
                                                                               rhs=wd[:, f, n * 512:(n + 1) * 512],
                                                                               start=(f == 0), stop=(f == 15)),
                             reads=[b_uT, b_wd], writes=[b_ps])
                    p.op("dve", lambda e, ps=ps, n=n: e.tensor_tensor(out=xoi[:, n * 512:(n + 1) * 512], in0=ps[:],
                                                                      in1=xsi[:, n * 512:(n + 1) * 512], op=ALU.add),
                         reads=[b_ps, b_xs, b_xo], writes=[b_xo])
                p.dma("sp", dst[i * 128:(i + 1) * 128, :], xoi[:], reads=[b_xo], writes=[b_mid[i] if ps_ == 0 else b_x2])
    p.wait_all("sp", [b_x2])
    p.emit()
    return nc


def build_SWA():
    nc = new_nc()
    sqT = dram(nc, "sqT", [384, NT], BF16, "ExternalInput")
    skT = dram(nc, "skT", [128, NT + 128], BF16, "ExternalInput")
    sv = dram(nc, "sv", [NT + 128, 128], BF16, "ExternalInput")
    sinks = dram(nc, "sinks", [6], F32, "ExternalInput")
    masks = dram(nc, "masks", [128, 3, 128], BF16, "ExternalInput")
    sel = dram(nc, "sel", [65, 64], F32, "ExternalInput")
    ysT = dram(nc, "ysT", [384, NT], BF16, "ExternalOutput")
    p = Prog(nc)
    QA, b_QA = p.tile("QA", [128, 3, NT], BF16)
    for g in range(2):
        for j in range(3):
            h = 3 * g + j
            p.dma("sp", QA[g * 64:(g + 1) * 64, j, :], sqT[h * 64:(h + 1) * 64, :], writes=[b_QA])
    KT, b_KT = p.tile("KT", [128, NT + 128], BF16)
    p.dma("sp", KT[:], skT[:, :], writes=[b_KT])
    NB = NT // 128
    VA, b_VA = p.tile("VA", [128, NB + 1, 2, 65], BF16)
    p.op("pool", lambda e: e.memset(VA[:], 1.0), writes=[b_VA])
    svv = sv.rearrange("(n p) (g d) -> p n g d", p=128, g=2)
    for n in range(NB + 1):
        p.dma("sp", VA[:, n, :, 0:64], svv[:, n, :, :], writes=[b_VA])
    MK, b_MK = p.tile("MK", [128, 3, 128], BF16)
    p.dma("sp", MK[:], masks[:, :, :], writes=[b_MK])
    esk, b_esk = p.tile("esk", [65, 6], F32)
    p.dma("sp", esk[:], sinks.partition_broadcast(65), writes=[b_esk])
    p.op("act", lambda e: e.activation(out=esk[:], in_=esk[:], func=AF.Exp), reads=[b_esk], writes=[b_esk])
    selt, b_sel = p.tile("selt", [65, 64], F32)
    p.dma("sp", selt[:], sel[:, :], writes=[b_sel])
    yall, b_yall = p.tile("yall", [64, 6, NT], BF16)
    PS = [(p.psum(f"ps{j}", [128, 512], F32), p.pbuf(f"ps{j}")) for j in range(8)]
    P1 = [p.tile(f"P1_{j}", [128, 384], BF16) for j in range(2)]
    P2 = [p.tile(f"P2_{j}", [128, 384], BF16) for j in range(2)]
    OSB = [p.tile(f"osb{j}", [65, 384], F32) for j in range(2)]
    RD = [p.tile(f"rd{j}", [64, 384], F32) for j in range(2)]
    b_out = p.buf("ysT")
    v3 = lambda ap: ap.rearrange("p (j q) -> p j q", j=3)
    it = 0
    for i in range(NB):
        for g in range(2):
            par = it % 2
            it += 1
            (ps1, b_ps1), (ps2, b_ps2) = PS[par], PS[2 + par]
            (pso, b_pso), (psb, b_psb) = PS[4 + par], PS[6 + par]
            (p1, b_p1), (p2, b_p2) = P1[par], P2[par]
            (osb, b_osb), (rd, b_rd) = OSB[par], RD[par]
            gs = slice(g * 64, (g + 1) * 64)
            rhs = QA[gs, :, i * 128:(i + 1) * 128]
            p.op("pe", lambda e: e.matmul(v3(ps1[:, 0:384]), lhsT=KT[gs, i * 128:(i + 1) * 128], rhs=rhs, start=True, stop=True),
                 reads=[b_KT, b_QA], writes=[b_ps1])
            p.op("pe", lambda e: e.matmul(v3(ps2[:, 0:384]), lhsT=KT[gs, (i + 1) * 128:(i + 2) * 128], rhs=rhs, start=True, stop=True),
                 reads=[b_KT, b_QA], writes=[b_ps2])
            p.op("act", lambda e: e.activation(out=p1[:], in_=ps1[:, 0:384], func=AF.Exp, scale=0.125), reads=[b_ps1], writes=[b_p1])
            p.op("act", lambda e: e.activation(out=p2[:], in_=ps2[:, 0:384], func=AF.Exp, scale=0.125), reads=[b_ps2], writes=[b_p2])
            mi = 2 if i == 0 else 0
            p.op("dve", lambda e: e.tensor_tensor(out=v3(p1[:]), in0=v3(p1[:]), in1=MK[:, mi, :].unsqueeze(1).to_broadcast([128, 3, 128]),
                                                  op=ALU.mult), reads=[b_p1, b_MK], writes=[b_p1])
            p.op("pool", lambda e: e.tensor_tensor(out=v3(p2[:]), in0=v3(p2[:]), in1=MK[:, 1, :].unsqueeze(1).to_broadcast([128, 3, 128]),
                                                   op=ALU.mult), reads=[b_p2, b_MK], writes=[b_p2])
            p.op("pe", lambda e: e.matmul(pso[0:65, 0:384], lhsT=VA[:, i, g, :], rhs=p1[:], start=True, stop=False),
                 reads=[b_VA, b_p1], writes=[b_pso])
            p.op("pe", lambda e: e.matmul(pso[0:65, 0:384], lhsT=VA[:, i + 1, g, :], rhs=p2[:], start=False, stop=True),
                 reads=[b_VA, b_p2], writes=[b_pso])
            p.op("act", lambda e: e.copy(out=osb[:], in_=pso[0:65, 0:384]), reads=[b_pso], writes=[b_osb])
            p.op("dve", lambda e: e.tensor_tensor(out=v3(osb[64:65, :]), in0=v3(osb[64:65, :]),
                                                  in1=esk[64:65, 3 * g:3 * g + 3].unsqueeze(2).to_broadcast([1, 3, 128]), op=ALU.add),
                 reads=[b_osb, b_esk], writes=[b_osb])
            p.op("pe", lambda e: e.matmul(psb[0:64, 0:384], lhsT=selt[:], rhs=osb[:], start=True, stop=True),
                 reads=[b_sel, b_osb], writes=[b_psb])
            p.op("dve", lambda e: e.reciprocal(out=rd[:], in_=psb[0:64, 0:384]), reads=[b_psb], writes=[b_rd])
            p.op("pool", lambda e: e.tensor_tensor(out=yall[:, 3 * g:3 * g + 3, i * 128:(i + 1) * 128], in0=v3(osb[0:64, :]),
                                                   in1=v3(rd[:]), op=ALU.mult), reads=[b_osb, b_rd, b_yall], writes=[b_yall])
    for h in range(6):
        p.dma("sp", ysT[h * 64:(h + 1) * 64, :], yall[:, h, :], reads=[b_yall], writes=[b_out])
    p.wait_all("sp", [b_out])
    p.emit()
    return nc


ML_NSEG = 4
ML_CPS = 32
ML_SEG = ML_CPS * 128


def build_ML(stop=99):
    nc = new_nc()
    NCH = S // 128
    SKIP = os.environ.get('ML_SKIP', '')
    MS = 'dve' if 'm' in SKIP else 'pool'
    xq = dram(nc, "xq", [64, S + 3], F32, "ExternalInput")
    xk = dram(nc, "xk", [64, S + 3], F32, "ExternalInput")
    cw = dram(nc, "cw", [64, 2, 5], F32, "ExternalInput")
    Vd = dram(nc, "V", [128, NCH, 64], BF16, "ExternalInput")
    XOd = dram(nc, "XO", [128, NCH, 64], F32, "ExternalInput")
    GId = dram(nc, "GI", [128, NCH], F32, "ExternalInput")
    GFd = dram(nc, "GF", [128, NCH], F32, "ExternalInput")
    gbd = dram(nc, "gb", [2], F32, "ExternalInput")
    ngd = dram(nc, "ng", [64], F32, "ExternalInput")
    cst = dram(nc, "cst", [128, 3, 128], F32, "ExternalInput")
    yT = dram(nc, "yT", [64, S], BF16, "ExternalOutput")
    p = Prog(nc)
    CS, b_CS = p.tile("CS", [128, 3, 128], F32)
    p.dma("sp", CS[:], cst[:, :, :], writes=[b_CS])
    ident, b_id = make_ident(p)
    CW, b_CW = p.tile("CW", [64, 2, 5], F32)
    p.dma("sp", CW[:], cw[:, :, :], writes=[b_CW])
    GI, b_GI = p.tile("GI", [128, NCH], F32)
    GF, b_GF = p.tile("GF", [128, NCH], F32)
    p.dma("sp", GI[:], GId[:, :], writes=[b_GI])
    p.dma("sp", GF[:], GFd[:, :], writes=[b_GF])
    gbt, b_gbt = p.tile("gbt", [128, 2], F32)
    p.dma("sp", gbt[:], gbd.partition_broadcast(128), writes=[b_gbt])
    NG, b_NG = p.tile("NG", [128, 64], F32)
    p.dma("sp", NG[:], ngd.partition_broadcast(128), writes=[b_NG])
    nfb, b_nfb = p.tile("nfb", [128, 1], F32)
    p.op("dve", lambda e: e.tensor_scalar(out=nfb[:], in0=gbt[:, 1:2], scalar1=-1.0, scalar2=None, op0=ALU.mult),
         reads=[b_gbt], writes=[b_nfb])
    PS = [(p.psum(f"ps{j}", [128, 512], F32), p.pbuf(f"ps{j}")) for j in range(6)]
    PT = [(p.psum(f"pt{j}", [128, 1024], BF16), p.pbuf(f"pt{j}")) for j in range(2)]
    _real_op = p.op
    if 'g' in SKIP:
        p.op = lambda *a_, **k_: None
    LN, b_LN = p.tile("LN", [128, NCH], F32)
    p.op("act", lambda e: e.activation(out=LN[:], in_=GF[:], func=AF.Exp, scale=-1.0, bias=nfb[:, 0:1]),
         reads=[b_GF, b_nfb], writes=[b_LN])
    p.op("act", lambda e: e.activation(out=LN[:], in_=LN[:], func=AF.Ln, bias=1.0), reads=[b_LN], writes=[b_LN])
    psB, b_psB = PS[0]
    psA, b_psA = PS[1]
    p.op("pe", lambda e: e.matmul(psB[:, 0:NCH], lhsT=CS[:, 0, :], rhs=LN[:], start=True, stop=True),
         reads=[b_CS, b_LN], writes=[b_psB])
    p.op("pe", lambda e: e.matmul(psA[0:64, 0:NCH], lhsT=CS[:, 1, 0:64], rhs=LN[:], start=True, stop=True),
         reads=[b_CS, b_LN], writes=[b_psA])
    E, b_E = p.tile("E", [128, NCH], F32)
    EA, b_EA = p.tile("EA", [64, NCH], F32)
    U, b_U = p.tile("U", [128, NCH], F32)
    p.op("act", lambda e: e.activation(out=E[:], in_=psB[:, 0:NCH], func=AF.Exp), reads=[b_psB], writes=[b_E])
    p.op("act", lambda e: e.activation(out=EA[:], in_=psA[0:64, 0:NCH], func=AF.Exp), reads=[b_psA], writes=[b_EA])
    p.op("dve", lambda e: e.scalar_tensor_tensor(out=U[:], in0=GI[:], scalar=gbt[:, 0:1], in1=psB[:, 0:NCH],
                                                 op0=ALU.add, op1=ALU.subtract), reads=[b_GI, b_gbt, b_psB], writes=[b_U])
    p.op("act", lambda e: e.activation(out=U[:], in_=U[:], func=AF.Exp), reads=[b_U], writes=[b_U])
    p.op = _real_op
    if stop <= 1:
        p.emit()
        return nc
    C32 = [p.tile(f"C32_{j}", [64, 66], F32) for j in range(2)]
    T1, b_T1 = p.tile("T1", [64, 66], F32)
    Cbf = [p.tile(f"Cbf{j}", [128, ML_CPS, 66], BF16) for j in range(2)]
    p.op("dve", lambda e: e.memset(C32[0][0][:], 0.0), writes=[C32[0][1]])
    p.op(MS, lambda e: e.memset(Cbf[0][0][:], 0.0), writes=[Cbf[0][1]])
    p.op(MS, lambda e: e.memset(Cbf[1][0][:], 0.0), writes=[Cbf[1][1]])
    RAW, b_RAW = p.tile("RAW", [64, ML_SEG + 3], F32)
    ACC, b_ACC = p.tile("ACC", [64, ML_SEG], F32)
    QT, b_QT = p.tile("QT", [128, ML_SEG], BF16)
    p.op(MS, lambda e: e.memset(QT[:], 0.0), writes=[b_QT])
    KT, b_KT = p.tile("KT", [128, ML_SEG], BF16)
    p.op(MS, lambda e: e.memset(KT[:], 0.0), writes=[b_KT])
    KTOK, b_KTOK = p.tile("KTOK", [128, ML_CPS, 64], BF16)
    VS, b_VS = p.tile("VS", [128, ML_CPS, 64], BF16)
    VP, b_VP = p.tile("VP", [128, ML_CPS, 66], BF16)
    p.op(MS, lambda e: e.memset(VP[:], 0.0), writes=[b_VP])
    AT, b_AT = p.tile("AT", [128, ML_CPS, 128], BF16)
    NSB, b_NSB = p.tile("NSB", [128, ML_CPS, 66], F32)
    XO, b_XO = p.tile("XO", [128, ML_CPS, 64], F32)
    H, b_H = p.tile("H", [128, ML_CPS, 64], F32)
    SQ, b_SQ = p.tile("SQ", [128, ML_CPS, 64], F32)
    DN, b_DN = p.tile("DN", [128, ML_CPS], F32)
    SSQ, b_SSQ = p.tile("SSQ", [128, ML_CPS], F32)
    RST, b_RST = p.tile("RST", [128, ML_CPS], F32)
    Y, b_Y = p.tile("Y", [128, ML_CPS, 128], BF16)
    p.op(MS, lambda e: e.memset(Y[:], 0.0), writes=[b_Y])
    YT, b_YT = p.tile("YT", [64, ML_SEG], BF16)
    b_out = p.buf("yT")
    def _dummy_io(k):
        if k <= 3:
            p.dma("sp", VS[:], Vd[:, 0:ML_CPS, :], writes=[b_VS])
            p.dma("sp", XO[:], XOd[:, 0:ML_CPS, :], writes=[b_XO])
        p.dma("sp", yT[:, 0:ML_SEG], YT[:], reads=[b_YT], writes=[b_out])

    cur = 0
    cnt_ps = 0
    for g in range(ML_NSEG):
        t0 = g * ML_SEG
        c0g = g * ML_CPS
        cbf, b_cbf = Cbf[g % 2]
        cbfn, b_cbfn = Cbf[(g + 1) % 2]
        if 'c' in SKIP:
            p.op = lambda *a_, **k_: None
        for which, (src, dst, b_dst) in enumerate(((xq, QT, b_QT), (xk, KT, b_KT))):
            p.dma("sp", RAW[:], src[:, t0:t0 + ML_SEG + 3], writes=[b_RAW])
            p.op("dve", lambda e, w=which: e.tensor_scalar(out=ACC[:], in0=RAW[:, 3:ML_SEG + 3], scalar1=CW[:, w, 3:4],
                                                           scalar2=CW[:, w, 4:5], op0=ALU.mult, op1=ALU.add),
                 reads=[b_RAW, b_CW], writes=[b_ACC])
            for j in (2, 1, 0):
                p.op("dve", lambda e, w=which, j=j: e.scalar_tensor_tensor(out=ACC[:], in0=RAW[:, j:ML_SEG + j], scalar=CW[:, w, j:j + 1],
                                                                           in1=ACC[:], op0=ALU.mult, op1=ALU.add),
                     reads=[b_RAW, b_CW, b_ACC], writes=[b_ACC])
            p.op("act", lambda e, dst=dst: e.activation(out=dst[0:64, :], in_=ACC[:], func=AF.Silu), reads=[b_ACC], writes=[b_dst])
        p.op = _real_op
        if stop <= 2:
            _dummy_io(2)
            break
        for cb in range(ML_CPS // 8):
            pt, b_pt = PT[cb % 2]
            for j in range(8):
                c = cb * 8 + j
                p.op("pe", lambda e, c=c, j=j, pt=pt: e.transpose(out=pt[:, j * 128:(j + 1) * 128], in_=KT[:, c * 128:(c + 1) * 128],
                                                                   identity=ident[:]), reads=[b_KT, b_id], writes=[b_pt])
            p.op("act", lambda e, cb=cb, pt=pt: e.copy(out=KTOK[:, cb * 8:(cb + 1) * 8, :],
                                                       in_=pt[:].rearrange("p (c d) -> p c d", c=8)[:, :, 0:64]),
                 reads=[b_pt], writes=[b_KTOK])
        if stop <= 3:
            _dummy_io(3)
            break
        p.dma("sp", VS[:], Vd[:, c0g:c0g + ML_CPS, :], writes=[b_VS])
        p.dma("sp", XO[:], XOd[:, c0g:c0g + ML_CPS, :], writes=[b_XO])
        Useg = U[:, c0g:c0g + ML_CPS]
        Eseg = E[:, c0g:c0g + ML_CPS]
        p.op("dve", lambda e: e.tensor_tensor(out=VP[:, :, 0:64], in0=VS[:], in1=Useg.unsqueeze(2).to_broadcast([128, ML_CPS, 64]),
                                              op=ALU.mult), reads=[b_VS, b_U, b_VP], writes=[b_VP])
        p.op("dve", lambda e: e.tensor_copy(out=VP[:, :, 64:65], in_=Useg.unsqueeze(2)), reads=[b_U, b_VP], writes=[b_VP])
        if stop <= 4:
            _dummy_io(4)
            break
        for c4 in range(ML_CPS // 4):
            psS, b_psS = PS[cnt_ps % 2]
            psC, b_psC = PS[2 + cnt_ps % 2]
            cnt_ps += 1
            for j in range(4):
                c = c4 * 4 + j
                cs_ = slice(c * 128, (c + 1) * 128)
                p.op("pe", lambda e, j=j, cs_=cs_, psS=psS: e.matmul(psS[:, j * 128:(j + 1) * 128], lhsT=KT[0:64, cs_], rhs=QT[0:64, cs_],
                                                                      start=True, stop=True), reads=[b_KT, b_QT], writes=[b_psS])
            p.op("dve", lambda e, c4=c4, psS=psS: e.tensor_tensor(out=AT[:, c4 * 4:(c4 + 1) * 4, :],
                                                                  in0=psS[:].rearrange("p (j q) -> p j q", j=4),
                                                                  in1=CS[:, 2, :].unsqueeze(1).to_broadcast([128, 4, 128]), op=ALU.mult),
                 reads=[b_psS, b_CS, b_AT], writes=[b_AT])
            if stop <= 4.3:
                continue
            for j in range(4):
                c = c4 * 4 + j
                p.op("pe", lambda e, j=j, c=c, psC=psC: e.matmul(psC[0:64, j * 128:j * 128 + 66], lhsT=KTOK[:, c, :], rhs=VP[:, c, 0:66],
                                                                  start=True, stop=True), reads=[b_KTOK, b_VP], writes=[b_psC])
            if stop <= 4.6:
                continue
            for j in range(4):
                c = c4 * 4 + j
                (cc, b_cc), (cn, b_cn) = C32[cur], C32[1 - cur]
                cur = 1 - cur
                MLV = int(os.environ.get("MLV", "0"))
                if MLV == 3:
                    continue
                p.op("dve", lambda e, j=j, cc=cc, psC=psC: e.scalar_tensor_tensor(out=T1[:], in0=psC[0:64, j * 128:j * 128 + 66], scalar=0.125,
                                                                                  in1=cc[:], op0=ALU.mult, op1=ALU.add),
                     reads=[b_psC, b_cc], writes=[b_T1])
                cg = c0g + c
                if MLV == 2:
                    continue
                p.op("dve", lambda e, cn=cn, cg=cg: e.tensor_scalar(out=cn[:], in0=T1[:], scalar1=EA[:, cg:cg + 1], scalar2=None, op0=ALU.mult),
                     reads=[b_T1, b_EA], writes=[b_cn])
                if MLV == 1:
                    continue
                if c + 1 < ML_CPS:
                    p.op("act", lambda e, cn=cn, c=c: e.copy(out=cbf[0:64, c + 1, 0:66], in_=cn[:]), reads=[b_cn, b_cbf], writes=[b_cbf])
                else:
                    p.op("act", lambda e, cn=cn: e.copy(out=cbfn[0:64, 0, 0:66], in_=cn[:]), reads=[b_cn, b_cbfn], writes=[b_cbfn])
        if stop <= 5:
            _dummy_io(5)
            break
        for c4 in range(ML_CPS // 4):
            psN, b_psN = PS[4 + c4 % 2]
            for j in range(4):
                c = c4 * 4 + j
                cs_ = slice(c * 128, (c + 1) * 128)
                p.op("pe", lambda e, j=j, c=c, psN=psN: e.matmul(psN[:, j * 128:j * 128 + 66], lhsT=AT[:, c, :], rhs=VP[:, c, 0:66],
                                                                  start=True, stop=False), reads=[b_AT, b_VP], writes=[b_psN])
                p.op("pe", lambda e, j=j, c=c, cs_=cs_, psN=psN: e.matmul(psN[:, j * 128:j * 128 + 66], lhsT=QT[:, cs_], rhs=cbf[:, c, 0:66],
                                                                           start=False, stop=True), reads=[b_QT, b_cbf], writes=[b_psN])
            p.op("act", lambda e, c4=c4, psN=psN: e.copy(out=NSB[:, c4 * 4:(c4 + 1) * 4, :], in_=psN[:].rearrange("p (c d) -> p c d", c=4)[:, :, 0:66]),
                 reads=[b_psN, b_NSB], writes=[b_NSB])
        if stop <= 6:
            _dummy_io(6)
            break
        p.op("act", lambda e: e.activation(out=DN[:].unsqueeze(2), in_=NSB[:, :, 64:65], func=AF.Abs), reads=[b_NSB], writes=[b_DN])
        p.op("dve", lambda e: e.tensor_tensor(out=DN[:], in0=DN[:], in1=Eseg, op=ALU.mult), reads=[b_DN, b_E], writes=[b_DN])
        p.op("dve", lambda e: e.tensor_scalar(out=DN[:], in0=DN[:], scalar1=1.0, scalar2=None, op0=ALU.max), reads=[b_DN], writes=[b_DN])
        p.op("dve", lambda e: e.reciprocal(out=DN[:], in_=DN[:]), reads=[b_DN], writes=[b_DN])
        p.op("dve", lambda e: e.tensor_tensor(out=DN[:], in0=DN[:], in1=Eseg, op=ALU.mult), reads=[b_DN, b_E], writes=[b_DN])
        p.op("act", lambda e: e.activation(out=XO[:], in_=XO[:], func=AF.Sigmoid), reads=[b_XO], writes=[b_XO])
        p.op("dve", lambda e: e.tensor_tensor(out=H[:], in0=NSB[:, :, 0:64], in1=DN[:].unsqueeze(2).to_broadcast([128, ML_CPS, 64]),
                                              op=ALU.mult), reads=[b_NSB, b_DN], writes=[b_H])
        p.op("pool", lambda e: e.tensor_tensor(out=H[:], in0=H[:], in1=XO[:], op=ALU.mult), reads=[b_H, b_XO], writes=[b_H])
        p.op("pool", lambda e: e.tensor_tensor(out=SQ[:], in0=H[:], in1=H[:], op=ALU.mult), reads=[b_H], writes=[b_SQ])
        p.op("dve", lambda e: e.tensor_reduce(out=SSQ[:], in_=SQ[:], axis=AX.X, op=ALU.add), reads=[b_SQ], writes=[b_SSQ])
        rms_rstd(p, SSQ[:], b_SSQ, RST[:], b_RST, 64)
        p.op("dve", lambda e: e.tensor_tensor(out=SQ[:], in0=H[:], in1=RST[:].unsqueeze(2).to_broadcast([128, ML_CPS, 64]), op=ALU.mult),
             reads=[b_H, b_RST, b_SQ], writes=[b_SQ])
        p.op("pool", lambda e: e.tensor_tensor(out=Y[:, :, 0:64], in0=SQ[:], in1=NG[:].unsqueeze(1).to_broadcast([128, ML_CPS, 64]), op=ALU.mult),
             reads=[b_SQ, b_NG, b_Y], writes=[b_Y])
        for cb in range(ML_CPS // 8):
            pt, b_pt = PT[cb % 2]
            for j in range(8):
                c = cb * 8 + j
                p.op("pe", lambda e, c=c, j=j, pt=pt: e.transpose(out=pt[:, j * 128:(j + 1) * 128], in_=Y[:, c, :], identity=ident[:]),
                     reads=[b_Y, b_id], writes=[b_pt])
            p.op("act", lambda e, cb=cb, pt=pt: e.copy(out=YT[:, cb * 1024:(cb + 1) * 1024], in_=pt[0:64, :]), reads=[b_pt, b_YT], writes=[b_YT])
        p.dma("sp", yT[:, t0:t0 + ML_SEG], YT[:], reads=[b_YT], writes=[b_out])
    p.wait_all("sp", [b_out])
    p.emit()
    return nc


MB_UNITS = 3
MB_SLOTS = 32


def build_MOBA():
    nc = new_nc()
    NKT = S // 128
    KTd = dram(nc, "KTa", [MB_UNITS, 128, S], BF16, "ExternalInput")
    Vd = dram(nc, "V", [MB_UNITS, 128, NKT, 64], BF16, "ExternalInput")
    QTd = dram(nc, "QT", [MB_UNITS, 64, MB_SLOTS * 256], BF16, "ExternalInput")
    pvd = dram(nc, "pv", [MB_UNITS, MB_SLOTS * 64], F32, "ExternalInput")
    ownd = dram(nc, "own", [MB_UNITS, MB_SLOTS * 64], F32, "ExternalInput")
    mld = dram(nc, "ml", [MB_UNITS, 128, 2, 4, 256], BF16, "ExternalInput")
    seld = dram(nc, "sel", [65, 64], F32, "ExternalInput")
    yTd = dram(nc, "yT", [MB_UNITS, 64, MB_SLOTS * 256], BF16, "ExternalOutput")
    p = Prog(nc)
    ident, b_id = make_ident(p)
    selt, b_sel = p.tile("selt", [65, 64], F32)
    p.dma("sp", selt[:], seld[:, :], writes=[b_sel])
    KTA, b_KTA = p.tile("KTA", [128, S], BF16)
    VA, b_VA = p.tile("VA", [128, NKT, 66], BF16)
    QA, b_QA = p.tile("QA", [128, MB_SLOTS * 256], BF16)
    PV, b_PV = p.tile("PV", [128, MB_SLOTS, 64], F32)
    OWN, b_OWN = p.tile("OWN", [128, MB_SLOTS, 64], F32)
    PB, b_PB = p.tile("PB", [128, MB_SLOTS, 64], F32)
    ML, b_ML = p.tile("ML", [128, 2, 4, 256], BF16)
    KM32, b_KM32 = p.tile("KM32", [64, 64], F32)
    KM, b_KM = p.tile("KM", [64, 64], BF16)
    STG, b_STG = p.tile("STG", [128, 128], BF16)
    p.op("dve", lambda e: e.memset(STG[:], 0.0), writes=[b_STG])
    YA, b_YA = p.tile("YA", [64, MB_SLOTS * 256], BF16)
    GM = [p.tile(f"GM{j}", [128, 64], F32) for j in range(2)]
    T8 = [p.tile(f"T8{j}", [128, 8], F32) for j in range(2)]
    T2 = [p.tile(f"T2{j}", [128, 64], F32) for j in range(2)]
    PT = [p.tile(f"PT{j}", [128, 256], BF16) for j in range(3)]
    OSB = [p.tile(f"OSB{j}", [65, 256], F32) for j in range(2)]
    RD = [p.tile(f"RD{j}", [64, 256], F32) for j in range(2)]
    PSg = [(p.psum(f"psg{j}", [128, 512], F32), p.pbuf(f"psg{j}")) for j in range(1)]
    PSt = [(p.psum(f"pst{j}", [128, 1024], BF16), p.pbuf(f"pst{j}")) for j in range(1)]
    PSs = [(p.psum(f"pss{j}", [128, 512], F32), p.pbuf(f"pss{j}")) for j in range(3)]
    PSo = [(p.psum(f"pso{j}", [128, 512], F32), p.pbuf(f"pso{j}")) for j in range(2)]
    PSb = [(p.psum(f"psb{j}", [128, 512], F32), p.pbuf(f"psb{j}")) for j in range(1)]
    b_out = p.buf("yT")
    qtc = 0
    ktc = 0
    for u in range(MB_UNITS):
        for j in range(4):
            p.dma("sp", KTA[:, j * 4096:(j + 1) * 4096], KTd[u, :, j * 4096:(j + 1) * 4096], writes=[b_KTA])
        p.op("pool", lambda e: e.memset(VA[:], 1.0), writes=[b_VA])
        for j in range(8):
            p.dma("sp", VA[:, j * 16:(j + 1) * 16, 0:64], Vd[u, :, j * 16:(j + 1) * 16, :], writes=[b_VA])
        p.dma("sp", QA[0:64, :], QTd[u, :, :], writes=[b_QA])
        p.dma("sp", PV[:].rearrange("p s n -> p (s n)"), pvd[u].partition_broadcast(128), writes=[b_PV])
        p.dma("sp", OWN[:].rearrange("p s n -> p (s n)"), ownd[u].partition_broadcast(128), writes=[b_OWN])
        p.dma("sp", ML[:], mld[u], writes=[b_ML])
        p.op("dve", lambda e: e.tensor_scalar(out=PB[:], in0=PV[:], scalar1=-1.0, scalar2=BIG, op0=ALU.add, op1=ALU.mult),
             reads=[b_PV], writes=[b_PB])
        p.op("dve", lambda e: e.tensor_reduce(out=KM32[:], in_=KTA[0:64, :].rearrange("p (n k) -> p n k", k=256), axis=AX.X, op=ALU.add),
             reads=[b_KTA], writes=[b_KM32])
        p.op("dve", lambda e: e.tensor_scalar(out=KM[:], in0=KM32[:], scalar1=1.0 / 256, scalar2=None, op0=ALU.mult),
             reads=[b_KM32], writes=[b_KM])
        for s_ in range(MB_SLOTS):
            for qt in range(2):
                q0 = s_ * 256 + qt * 128
                par = qtc % 2
                qtc += 1
                (gm, b_gm), (t8, b_t8), (t2, b_t2) = GM[par], T8[par], T2[par]
                psg, b_psg = PSg[0]
                pst, b_pst = PSt[0]
                p.op("pe", lambda e: e.matmul(psg[:, 0:64], lhsT=QA[0:64, q0:q0 + 128], rhs=KM[:], start=True, stop=True),
                     reads=[b_QA, b_KM], writes=[b_psg])
                p.op("dve", lambda e: e.tensor_tensor(out=gm[:], in0=psg[:, 0:64], in1=PB[:, s_, :], op=ALU.add),
                     reads=[b_psg, b_PB], writes=[b_gm])
                p.op("dve", lambda e: e.max(out=t8[:], in_=gm[:]), reads=[b_gm], writes=[b_t8])
                p.op("dve", lambda e: e.scalar_tensor_tensor(out=t2[:], in0=gm[:], scalar=t8[:, 2:3], in1=PV[:, s_, :],
                                                             op0=ALU.is_ge, op1=ALU.mult), reads=[b_gm, b_t8, b_PV], writes=[b_t2])
                p.op("pool", lambda e: e.tensor_tensor(out=t2[:], in0=t2[:], in1=OWN[:, s_, :], op=ALU.add),
                     reads=[b_t2, b_OWN], writes=[b_t2])
                p.op("pool", lambda e: e.tensor_scalar(out=STG[:, 64:128], in0=t2[:], scalar1=BIG, scalar2=-BIG, op0=ALU.mult, op1=ALU.add),
                     reads=[b_t2, b_STG], writes=[b_STG])
                p.op("pe", lambda e: e.transpose(out=pst[:, 0:128], in_=STG[:], identity=ident[:]), reads=[b_STG, b_id], writes=[b_pst])
                p.op("act", lambda e: e.copy(out=QA[64:128, q0:q0 + 128], in_=pst[64:128, 0:128]), reads=[b_pst, b_QA], writes=[b_QA])
            nk = 2 * (2 * s_ + 2)
            pso, b_pso = PSo[s_ % 2]
            for kt in range(nk):
                pss, b_pss = PSs[ktc % 3]
                pt, b_pt = PT[ktc % 3]
                ktc += 1
                p.op("pe", lambda e: e.matmul(pss[:, 0:256], lhsT=KTA[:, kt * 128:(kt + 1) * 128], rhs=QA[:, s_ * 256:(s_ + 1) * 256],
                                              start=True, stop=True), reads=[b_KTA, b_QA], writes=[b_pss])
                p.op("act", lambda e: e.activation(out=pt[:], in_=pss[:, 0:256], func=AF.Exp, scale=0.125), reads=[b_pss], writes=[b_pt])
                if kt >= nk - 4:
                    pos = kt - (nk - 4)
                    p.op("dve", lambda e: e.tensor_tensor(out=pt[:], in0=pt[:], in1=ML[:, s_ % 2, pos, :], op=ALU.mult),
                         reads=[b_pt, b_ML], writes=[b_pt])
                p.op("pe", lambda e: e.matmul(pso[0:65, 0:256], lhsT=VA[:, kt, 0:65], rhs=pt[:], start=(kt == 0), stop=(kt == nk - 1)),
                     reads=[b_VA, b_pt], writes=[b_pso])
            (osb, b_osb), (rd, b_rd) = OSB[s_ % 2], RD[s_ % 2]
            psb, b_psb = PSb[0]
            p.op("dve", lambda e: e.tensor_copy(out=osb[:], in_=pso[0:65, 0:256]), reads=[b_pso], writes=[b_osb])
            p.op("pe", lambda e: e.matmul(psb[0:64, 0:256], lhsT=selt[:], rhs=osb[:], start=True, stop=True),
                 reads=[b_sel, b_osb], writes=[b_psb])
            p.op("dve", lambda e: e.reciprocal(out=rd[:], in_=psb[0:64, 0:256]), reads=[b_psb], writes=[b_rd])
            p.op("pool", lambda e: e.tensor_tensor(out=YA[:, s_ * 256:(s_ + 1) * 256], in0=osb[0:64, :], in1=rd[:], op=ALU.mult),
                 reads=[b_osb, b_rd, b_YA], writes=[b_YA])
        p.dma("sp", yTd[u, :, :], YA[:], reads=[b_YA], writes=[b_out])
    p.emit()
    return nc


_CACHE = {}


def _get(name, fn):
    if name not in _CACHE:
        _CACHE[name] = fn()
    return _CACHE[name]


def _run(name, fn, in_maps):
    nc = _get(name, fn)
    res = run_bass_kernel_spmd(nc, in_maps, core_ids=list(range(NCORE)))
    return res.results


def _rope_tables():
    inv = (10000.0 ** (-np.arange(0, 64, 2, dtype=np.float32) / 64)).astype(np.float32)
    ang = np.arange(S, dtype=np.float32)[:, None] * inv[None, :]
    return np.cos(ang).astype(np.float32), np.sin(ang).astype(np.float32)


def tok_shard(a):
    return [np.ascontiguousarray(a[c // 4, (c % 4) * NT:(c % 4 + 1) * NT]) for c in range(NCORE)]


def tok_unshard(parts):
    return np.stack([np.concatenate(parts[0:4], 0), np.concatenate(parts[4:8], 0)], 0)


def swa_inputs(sq, sk_h, sv_h, sinks, first):
    kl = np.arange(128)[:, None]
    ql = np.arange(128)[None, :]
    mprev = (kl > ql).astype(np.float32)
    mown = (kl <= ql).astype(np.float32)
    mfirst = np.zeros_like(mprev) if first else mprev
    masks = np.ascontiguousarray(np.stack([mprev, mown, mfirst], 1)).astype(NPBF)
    sel = np.zeros((65, 64), np.float32)
    sel[64] = 1.0
    return dict(sqT=np.ascontiguousarray(sq.T), skT=np.ascontiguousarray(sk_h.T), sv=np.ascontiguousarray(sv_h),
                sinks=np.ascontiguousarray(sinks, dtype=np.float32), masks=masks, sel=sel)


def ml_consts():
    s_ = np.arange(128)[:, None]
    t_ = np.arange(128)[None, :]
    tri = (s_ <= t_).astype(np.float32)
    cst = np.zeros((128, 3, 128), np.float32)
    cst[:, 0] = -tri
    cst[:, 1, 0:64] = -1.0
    cst[:, 2] = tri * 0.125
    return cst


def ml_inputs(xqk_h, xv_h, xo_h, xi_h, xf_h, conv_w_h, conv_b_h, ib, fb, ng, cst):
    z = np.zeros((64, 3), np.float32)
    xq = np.ascontiguousarray(np.concatenate([z, xqk_h[:, 0:64].T], 1))
    xk = np.ascontiguousarray(np.concatenate([z, xqk_h[:, 64:128].T], 1))
    cw = np.zeros((64, 2, 5), np.float32)
    for w in range(2):
        cw[:, w, 0:4] = conv_w_h[:, w * 64:(w + 1) * 64].T
        cw[:, w, 4] = conv_b_h[w * 64:(w + 1) * 64]
    tc = lambda a: np.ascontiguousarray(a.reshape(S // 128, 128, *a.shape[1:]).swapaxes(0, 1))
    return dict(xq=xq, xk=xk, cw=cw, V=tc(xv_h), XO=tc(xo_h), GI=tc(xi_h), GF=tc(xf_h),
                gb=np.array([ib, fb], np.float32), ng=np.ascontiguousarray(ng, dtype=np.float32), cst=cst)


def moba_qb(s_, half):
    if half == 0:
        return 2 * s_ if s_ % 2 == 0 else 2 * s_ + 1
    return 2 * s_ + 1 if s_ % 2 == 0 else 2 * s_


def moba_consts():
    ind = (np.arange(S)[None, :] // 256 == np.arange(64)[:, None]).astype(np.float32).astype(NPBF)
    sel = np.zeros((65, 64), np.float32)
    sel[64] = 1.0
    kl = np.arange(128)[:, None]
    q = np.arange(256)[None, :]
    tri = [(p_ * 128 + kl <= q).astype(np.float32) for p_ in range(2)]
    ones = np.ones((128, 256), np.float32)
    zeros = np.zeros((128, 256), np.float32)
    own_first = np.stack([tri[0], tri[1], zeros, zeros], 0)
    own_second = np.stack([ones, ones, tri[0], tri[1]], 0)
    mls = []
    for half in range(2):
        par = []
        for parity in range(2):
            qb_is_first = (moba_qb(parity, half) == 2 * parity)
            par.append(own_first if qb_is_first else own_second)
        mls.append(np.ascontiguousarray(np.stack(par, 0).transpose(2, 0, 1, 3)).astype(NPBF))
    pvs, owns = [], []
    for half in range(2):
        pv = np.zeros((MB_SLOTS, 64), np.float32)
        own = np.zeros((MB_SLOTS, 64), np.float32)
        for s_ in range(MB_SLOTS):
            qb = moba_qb(s_, half)
            pv[s_, :qb] = 1.0
            own[s_, qb] = 1.0
        pvs.append(pv.reshape(-1))
        owns.append(own.reshape(-1))
    return dict(ind=ind, sel=sel, ml=mls, pv=pvs, own=owns)


def moba_unit_inputs(mq_h, mk_h, mv_h, half, cst):
    KTa = np.concatenate([mk_h.T, cst["ind"]], 0)
    V = mv_h.reshape(S // 128, 128, 64).swapaxes(0, 1)
    QT = np.concatenate([mq_h[moba_qb(s_, half) * 256:(moba_qb(s_, half) + 1) * 256].T for s_ in range(MB_SLOTS)], 1)
    return dict(KTa=KTa, V=V, QT=QT, pv=cst["pv"][half], own=cst["own"][half], ml=cst["ml"][half])


def moba_core_inputs(units, cst):
    d = {k: np.ascontiguousarray(np.stack([u[k] for u in units], 0)) for k in ("KTa", "V", "QT", "pv", "own", "ml")}
    d["sel"] = cst["sel"]
    return d


def moba_scatter(y, yT_u, half, h):
    for s_ in range(MB_SLOTS):
        qb = moba_qb(s_, half)
        y[qb * 256:(qb + 1) * 256, h * 64:(h + 1) * 64] = yT_u[:, s_ * 256:(s_ + 1) * 256].T


def _layer(xsh, l, P, consts):
    cos6, sin6, gains_all, mlc, mbc = consts
    resA = _run("A", build_A, [dict(x=xsh[c], w_in=P["w_in"][l], ln=P["ln1"][l], gains=gains_all[l],
                                    cos6=cos6[(c % 4) * NT:(c % 4 + 1) * NT], sin6=sin6[(c % 4) * NT:(c % 4 + 1) * NT])
                               for c in range(NCORE)])
    OBF = tok_unshard([r["obf"] for r in resA])
    OF = tok_unshard([r["of32"] for r in resA])
    maps = []
    for c in range(NCORE):
        b, qd = c // 4, c % 4
        t0 = qd * NT
        if qd == 0:
            kv = np.concatenate([np.zeros((128, 256), NPBF), OBF[b, 0:NT, 1536:1792]], 0)
        else:
            kv = OBF[b, t0 - 128:t0 + NT, 1536:1792]
        maps.append(swa_inputs(OBF[b, t0:t0 + NT, 1152:1536], kv[:, 0:128], kv[:, 128:256], P["swa_sinks"][l], qd == 0))
    resS = _run("SWA", build_SWA, maps)
    ycat = np.zeros((2, S, DM), NPBF)
    for c in range(NCORE):
        b, qd = c // 4, c % 4
        ycat[b, qd * NT:(qd + 1) * NT, 640:1024] = resS[c]["ysT"].T
    maps = []
    for c in range(NCORE):
        b, h = c // 4, c % 4
        xqk_h = np.concatenate([OF[b][:, h * 64:(h + 1) * 64], OF[b][:, 256 + h * 64:256 + (h + 1) * 64]], 1)
        cwh = np.concatenate([P["conv_w"][l][:, h * 64:(h + 1) * 64], P["conv_w"][l][:, 256 + h * 64:256 + (h + 1) * 64]], 1)
        cbh = np.concatenate([P["conv_b"][l][h * 64:(h + 1) * 64], P["conv_b"][l][256 + h * 64:256 + (h + 1) * 64]], 0)
        maps.append(ml_inputs(xqk_h, np.ascontiguousarray(OBF[b][:, 1792 + h * 64:1792 + (h + 1) * 64]),
                              np.ascontiguousarray(OF[b][:, 512 + h * 64:512 + (h + 1) * 64]),
                              np.ascontiguousarray(OF[b][:, 768 + h]), np.ascontiguousarray(OF[b][:, 772 + h]),
                              cwh, cbh, P["igate_b"][l][h], P["fgate_b"][l][h], P["mlstm_norm"][l][h], mlc))
    resM = _run("ML", build_ML, maps)
    for c in range(NCORE):
        b, h = c // 4, c % 4
        ycat[b, :, 384 + h * 64:384 + (h + 1) * 64] = resM[c]["yT"].T
    maps = []
    ulist = []
    for c in range(NCORE):
        us = []
        for j in range(MB_UNITS):
            u = c * MB_UNITS + j
            b, h, half = u // 12, (u % 12) // 2, u % 2
            ulist.append((b, h, half))
            us.append(moba_unit_inputs(OBF[b][:, h * 64:(h + 1) * 64], OBF[b][:, 384 + h * 64:384 + (h + 1) * 64],
                                       OBF[b][:, 768 + h * 64:768 + (h + 1) * 64], half, mbc))
        maps.append(moba_core_inputs(us, mbc))
    resB = _run("MOBA", build_MOBA, maps)
    for c in range(NCORE):
        for j in range(MB_UNITS):
            b, h, half = ulist[c * MB_UNITS + j]
            moba_scatter(ycat[b], resB[c]["yT"][j], half, h)
    resC1 = _run("C1", build_C1, [dict(x=xsh[c], yT=np.ascontiguousarray(ycat[c // 4, (c % 4) * NT:(c % 4 + 1) * NT].T),
                                       w_out=P["w_out"][l], ln=P["ln2"][l]) for c in range(NCORE)])
    resC2 = _run("C2", build_C2, [dict(x1=resC1[c]["x1"], hT=resC1[c]["hT"], w_up=P["w_up"][l], w_dn=P["w_down"][l])
                                  for c in range(NCORE)])
    return [r["x2"] for r in resC2]


def kernel(**inputs):
    P = {k: np.ascontiguousarray(np.asarray(v)) for k, v in inputs.items()}
    cos, sin = _rope_tables()
    cos6 = np.ascontiguousarray(np.tile(cos, (1, 6)))
    sin6 = np.ascontiguousarray(np.tile(sin, (1, 6)))
    depth = P["w_in"].shape[0]
    gains_all = [np.ascontiguousarray(np.stack([np.tile(P[k][l], 6) for k in ("moba_q_norm", "moba_k_norm", "swa_q_norm", "swa_k_norm")])
                                      .astype(np.float32)) for l in range(depth)]
    consts = (cos6, sin6, gains_all, ml_consts(), moba_consts())
    xsh = tok_shard(P["x"])
    for l in range(depth):
        xsh = _layer(xsh, l, P, consts)
    return tok_unshard(xsh).astype(np.float32)
```
